# Optimizing a Trainium2 kernel written in Bass

```python
import jax
import jax.numpy as jnp
from jax import lax
import numpy as np

D_MODEL = 2048
BATCH = 1
SEQ = 8192
DEPTH = 4

CTX_LEN = 256
GRID_W = 64
N_MIXERS = 2
N_S5_LAYERS = (DEPTH + 1) // 2
N_HGRN_LAYERS = DEPTH // 2
N_DIR = 2
EXPAND = 2
D_INNER = EXPAND * D_MODEL
S5_GROUP = 16
S5_GROUPS = D_INNER // S5_GROUP
S5_STATE = 64
S5_CHUNK = 128
S5_DT_MIN = 0.001
S5_DT_MAX = 0.1
HGRN_DK = 128
HGRN_DV = 128
HGRN_HEADS = D_INNER // HGRN_DV
HGRN_CHUNK = 64
EPS = 1e-6

kernel_name = 'hybrid_s5_hgrn2_prefix_dit_trunk'


def rmsnorm(x, gain):
    xf = x.astype(jnp.float32)
    inv = lax.rsqrt(jnp.mean(xf * xf, axis=-1, keepdims=True) + EPS)
    return (xf * inv).astype(x.dtype) * gain


def adaln(cond, w, b):
    mod = jax.nn.silu(cond) @ w + b
    return jnp.split(mod, 3, axis=-1)


def to_col_major(t, rows):
    b, l, d = t.shape
    return t.reshape(b, rows, GRID_W, d).transpose(0, 2, 1, 3).reshape(b, l, d)


def from_col_major(t, rows):
    b, l, d = t.shape
    return t.reshape(b, GRID_W, rows, d).transpose(0, 2, 1, 3).reshape(b, l, d)


def flip_seq(t):
    return jnp.flip(t, axis=1)


def same_order(t):
    return t


def s5_discretize(lam_re, lam_im, log_dt, b_re, b_im):
    lam_re, lam_im, log_dt, b_re, b_im = (t.astype(jnp.float32) for t in (lam_re, lam_im, log_dt, b_re, b_im))
    dt = jnp.exp(log_dt)[:, None]
    mag = jnp.exp(lam_re * dt)
    ar = mag * jnp.cos(lam_im * dt)
    ai = mag * jnp.sin(lam_im * dt)
    den = lam_re * lam_re + lam_im * lam_im
    qr = ((ar - 1.0) * lam_re + ai * lam_im) / den
    qi = (ai * lam_re - (ar - 1.0) * lam_im) / den
    bbr = qr[..., None] * b_re - qi[..., None] * b_im
    bbi = qr[..., None] * b_im + qi[..., None] * b_re
    return ar, ai, bbr, bbi


def s5_combine(e1, e2):
    ar1, ai1, br1, bi1 = e1
    ar2, ai2, br2, bi2 = e2
    return (ar2 * ar1 - ai2 * ai1, ar2 * ai1 + ai2 * ar1,
            ar2 * br1 - ai2 * bi1 + br2, ar2 * bi1 + ai2 * br1 + bi2)


def s5_scan(u, ar, ai, bbr, bbi, cr, ci, h0r, h0i, with_output):
    bsz, l = u.shape[:2]
    n = l // S5_CHUNK
    ub = u.reshape(bsz, n, S5_CHUNK, S5_GROUPS, S5_GROUP).transpose(1, 0, 2, 3, 4)
    a_shape = (bsz, S5_CHUNK, S5_GROUPS, S5_STATE)
    a_r = jnp.broadcast_to(ar, a_shape)
    a_i = jnp.broadcast_to(ai, a_shape)

    def step(carry, u_blk):
        hr, hi = carry
        bu_r = jnp.einsum('btgc,gpc->btgp', u_blk, bbr)
        bu_i = jnp.einsum('btgc,gpc->btgp', u_blk, bbi)
        pw_r, pw_i, s_r, s_i = lax.associative_scan(s5_combine, (a_r, a_i, bu_r, bu_i), axis=1)
        h_r = pw_r * hr[:, None] - pw_i * hi[:, None] + s_r
        h_i = pw_r * hi[:, None] + pw_i * hr[:, None] + s_i
        new = (h_r[:, -1], h_i[:, -1])
        if not with_output:
            return new, None
        y = jnp.einsum('btgp,gcp->btgc', h_r, cr) - jnp.einsum('btgp,gcp->btgc', h_i, ci)
        return new, y

    (hr, hi), ys = lax.scan(step, (h0r, h0i), ub)
    if not with_output:
        return None, hr, hi
    return ys.transpose(1, 0, 2, 3, 4).reshape(bsz, l, S5_GROUPS, S5_GROUP), hr, hi


def s5_branch(hx, hc, w_in, lam_re, lam_im, log_dt, b_re, b_im, c_re, c_im,
              d_skip, w_glu, b_glu, w_out, need_ctx):
    bsz = hx.shape[0]
    ux, zx = jnp.split(hx @ w_in, 2, axis=-1)
    pc = hc @ (w_in if need_ctx else w_in[:, :D_INNER])
    uc = pc[..., :D_INNER]

    def groups(t):
        return t.astype(jnp.float32).reshape(t.shape[0], t.shape[1], S5_GROUPS, S5_GROUP)

    gx, gc = groups(ux), groups(uc)
    zeros = jnp.zeros((bsz, S5_GROUPS, S5_STATE), jnp.float32)
    yx, yc = 0.0, 0.0
    for d in range(N_DIR):
        order = flip_seq if d == 1 else same_order
        ar, ai, bbr, bbi = s5_discretize(lam_re[d], lam_im[d], log_dt[d], b_re[d], b_im[d])
        cr, ci = c_re[d].astype(jnp.float32), c_im[d].astype(jnp.float32)
        y_c, hr, hi = s5_scan(order(gc), ar, ai, bbr, bbi, cr, ci, zeros, zeros, need_ctx)
        y_x, _, _ = s5_scan(order(gx), ar, ai, bbr, bbi, cr, ci, hr, hi, True)
        yx = yx + order(y_x)
        if need_ctx:
            yc = yc + order(y_c)

    def finish(y, u, z):
        y = y.reshape(u.shape).astype(u.dtype) + d_skip * u
        y = jax.nn.gelu(y, approximate=False)
        y = y * jax.nn.sigmoid(y @ w_glu + b_glu)
        return (y * jax.nn.silu(z)) @ w_out

    ox = finish(yx, ux, zx)
    oc = finish(yc, uc, pc[..., D_INNER:]) if need_ctx else None
    return ox, oc


def hgrn_lower_bound(lb_raw, j):
    p = jax.nn.softmax(lb_raw.astype(jnp.float32), axis=0)
    return jnp.cumsum(p, axis=0)[j] - p[0]


def hgrn_heads(t):
    return t.astype(jnp.float32).reshape(t.shape[0], t.shape[1], HGRN_HEADS, -1)


def hgrn_forget(f_raw, lb):
    lbh = lb.reshape(HGRN_HEADS, HGRN_DK)
    g = jnp.logaddexp(jnp.log(lbh), jnp.log1p(-lbh) + jax.nn.log_sigmoid(hgrn_heads(f_raw)))
    return g, -jnp.expm1(g)


def hgrn2_scan(k, v, g, s0, q=None):
    bsz, l = k.shape[:2]
    n = l // HGRN_CHUNK

    def blocks(t):
        return t.reshape(bsz, n, HGRN_CHUNK, HGRN_HEADS, t.shape[-1]).transpose(1, 0, 3, 2, 4)

    kb, vb = blocks(k), blocks(v)
    bb = jnp.cumsum(blocks(g), axis=3)
    lower = jnp.tril(jnp.ones((HGRN_CHUNK, HGRN_CHUNK), dtype=bool))[:, :, None]

    def step(s, blk):
        kc, vc, bc = blk[0], blk[1], blk[2]
        b_last = bc[:, :, -1, :]
        s_new = (jnp.exp(b_last)[..., None] * s
                 + jnp.einsum('bhsd,bhsv->bhdv', kc * jnp.exp(b_last[:, :, None] - bc), vc))
        if q is None:
            return s_new, None
        qc = blk[3]
        o_inter = jnp.einsum('bhtd,bhdv->bhtv', qc * jnp.exp(bc), s)
        decay = jnp.exp(jnp.where(lower, bc[:, :, :, None, :] - bc[:, :, None, :, :], -jnp.inf))
        scores = jnp.sum(qc[:, :, :, None, :] * kc[:, :, None, :, :] * decay, axis=-1)
        return s_new, o_inter + jnp.einsum('bhts,bhsv->bhtv', scores, vc)

    xs = (kb, vb, bb) if q is None else (kb, vb, bb, blocks(q))
    s_fin, ob = lax.scan(step, s0, xs)
    if q is None:
        return None, s_fin
    return ob.transpose(1, 0, 3, 2, 4).reshape(bsz, l, HGRN_HEADS, HGRN_DV), s_fin


def hgrn2_branch(hx, hc, w_in, lb_raw, o_norm, w_out, j, need_ctx):
    bsz = hx.shape[0]
    ix, f_fx, f_bx, qx, zx = jnp.split(hx @ w_in, 5, axis=-1)
    pc = hc @ (w_in if need_ctx else w_in[:, :3 * D_INNER])
    ic, f_fc, f_bc = pc[..., :D_INNER], pc[..., D_INNER:2 * D_INNER], pc[..., 2 * D_INNER:3 * D_INNER]
    v_x, v_c = hgrn_heads(ix), hgrn_heads(ic)
    q_x = hgrn_heads(jax.nn.silu(qx))
    q_c = hgrn_heads(jax.nn.silu(pc[..., 3 * D_INNER:4 * D_INNER])) if need_ctx else None
    s0 = jnp.zeros((bsz, HGRN_HEADS, HGRN_DK, HGRN_DV), jnp.float32)
    ox, oc = 0.0, 0.0
    for d, (f_x, f_c) in enumerate(((f_fx, f_fc), (f_bx, f_bc))):
        order = flip_seq if d == 1 else same_order
        lb = hgrn_lower_bound(lb_raw[d], j)
        g_x, k_x = hgrn_forget(f_x, lb)
        g_c, k_c = hgrn_forget(f_c, lb)
        o_c, s_c = hgrn2_scan(order(k_c), order(v_c), order(g_c), s0,
                              order(q_c) if need_ctx else None)
        o_x, _ = hgrn2_scan(order(k_x), order(v_x), order(g_x), s_c, order(q_x))
        ox = ox + order(o_x)
        if need_ctx:
            oc = oc + order(o_c)

    def finish(o, z):
        o = rmsnorm(o, o_norm).reshape(z.shape).astype(z.dtype)
        return (o * jax.nn.silu(z)) @ w_out

    out_x = finish(ox, zx)
    out_c = finish(oc, pc[..., 4 * D_INNER:]) if need_ctx else None
    return out_x, out_c


def setup_inputs(seed: int = 0) -> dict:
    key = jax.random.key(seed)
    ks = jax.random.split(key, 24)
    f32 = jnp.float32

    def nrm(k, shape, std):
        return std * jax.random.normal(k, shape, f32)

    e = D_INNER
    s5_shape = (N_S5_LAYERS, N_DIR, S5_GROUPS, S5_STATE)
    n_idx = jnp.arange(S5_STATE, dtype=f32)
    log_lo, log_hi = float(np.log(S5_DT_MIN)), float(np.log(S5_DT_MAX))
    return {
        'x': nrm(ks[0], (BATCH, SEQ, D_MODEL), 1.0),
        'c': nrm(ks[1], (BATCH, D_MODEL), 1.0),
        'ctx': nrm(ks[2], (BATCH, CTX_LEN, D_MODEL), 1.0),
        'c_ctx': nrm(ks[3], (D_MODEL,), 1.0),
        'ada_w': nrm(ks[4], (DEPTH, D_MODEL, 3 * D_MODEL), 0.5 * D_MODEL ** -0.5),
        'ada_b': nrm(ks[5], (DEPTH, 3 * D_MODEL), 0.02),
        'norm_pre': 1.0 + nrm(ks[6], (DEPTH, D_MODEL), 0.05),
        'norm_post': 1.0 + nrm(ks[7], (DEPTH, D_MODEL), 0.05),
        's5_w_in': nrm(ks[8], (N_S5_LAYERS, D_MODEL, 2 * e), D_MODEL ** -0.5),
        's5_lam_re': -0.5 + nrm(ks[9], s5_shape, 0.01),
        's5_lam_im': jnp.pi * n_idx + nrm(ks[10], s5_shape, 0.01),
        's5_log_dt': jax.random.uniform(ks[11], (N_S5_LAYERS, N_DIR, S5_GROUPS), f32, log_lo, log_hi),
        's5_b_re': nrm(ks[12], s5_shape + (S5_GROUP,), (2 * S5_GROUP) ** -0.5),
        's5_b_im': nrm(ks[13], s5_shape + (S5_GROUP,), (2 * S5_GROUP) ** -0.5),
        's5_c_re': nrm(ks[14], (N_S5_LAYERS, N_DIR, S5_GROUPS, S5_GROUP, S5_STATE), S5_STATE ** -0.5),
        's5_c_im': nrm(ks[15], (N_S5_LAYERS, N_DIR, S5_GROUPS, S5_GROUP, S5_STATE), S5_STATE ** -0.5),
        's5_d': nrm(ks[16], (N_S5_LAYERS, e), 1.0),
        's5_w_glu': nrm(ks[17], (N_S5_LAYERS, e, e), e ** -0.5),
        's5_b_glu': nrm(ks[18], (N_S5_LAYERS, e), 0.02),
        's5_w_out': nrm(ks[19], (N_S5_LAYERS, e, D_MODEL), e ** -0.5),
        'hgrn_w_in': nrm(ks[20], (N_HGRN_LAYERS, D_MODEL, 5 * e), D_MODEL ** -0.5),
        'hgrn_lb': nrm(ks[21], (N_DIR, N_HGRN_LAYERS, e), 0.5),
        'hgrn_norm': 1.0 + nrm(ks[22], (N_HGRN_LAYERS, HGRN_HEADS, HGRN_DV), 0.05),
        'hgrn_w_out': nrm(ks[23], (N_HGRN_LAYERS, e, D_MODEL), e ** -0.5),
    }


def reference(x, c, ctx, c_ctx, ada_w, ada_b, norm_pre, norm_post,
              s5_w_in, s5_lam_re, s5_lam_im, s5_log_dt, s5_b_re, s5_b_im,
              s5_c_re, s5_c_im, s5_d, s5_w_glu, s5_b_glu, s5_w_out,
              hgrn_w_in, hgrn_lb, hgrn_norm, hgrn_w_out):
    rows = x.shape[1] // GRID_W
    xc = ctx
    for i in range(DEPTH):
        need_ctx = i < DEPTH - 1
        j = i // N_MIXERS
        shift_x, scale_x, gate_x = adaln(c, ada_w[i], ada_b[i])
        shift_c, scale_c, gate_c = adaln(c_ctx, ada_w[i], ada_b[i])
        hx = rmsnorm(x, norm_pre[i]) * (1.0 + scale_x[:, None]) + shift_x[:, None]
        hc = rmsnorm(xc, norm_pre[i]) * (1.0 + scale_c) + shift_c
        col_major = j % 2 == 1
        if col_major:
            hx = to_col_major(hx, rows)
        if i % N_MIXERS == 0:
            ox, oc = s5_branch(hx, hc, s5_w_in[j], s5_lam_re[j], s5_lam_im[j], s5_log_dt[j],
                               s5_b_re[j], s5_b_im[j], s5_c_re[j], s5_c_im[j], s5_d[j],
                               s5_w_glu[j], s5_b_glu[j], s5_w_out[j], need_ctx)
        else:
            ox, oc = hgrn2_branch(hx, hc, hgrn_w_in[j], hgrn_lb, hgrn_norm[j], hgrn_w_out[j],
                                  j, need_ctx)
        if col_major:
            ox = from_col_major(ox, rows)
        x = x + gate_x[:, None] * rmsnorm(ox, norm_post[i])
        if need_ctx:
            xc = xc + gate_c * rmsnorm(oc, norm_post[i])
    return x
```

```python
import types
import numpy as np
import concourse.bass as bass
import concourse.mybir as mybir
from concourse.bass_utils import run_bass_kernel_spmd

F32 = mybir.dt.float32
BF16 = mybir.dt.bfloat16
AF = mybir.ActivationFunctionType
ALU = mybir.AluOpType
AX = mybir.AxisListType

NCORE = 8
D = 2048
E = 4096
EC = E // NCORE
NCC = EC // 128
L = 8192
CTX = 256
S = L + CTX
NT = S // 128
GRID_W = 64
EPS = 1e-6
MAGIC = 12582912.0
TWO_PI = float(2 * np.pi)
NLAYERS = 4


def _freeze(fn):
    if fn.__closure__ is None:
        return fn
    cells = []
    for c in fn.__closure__:
        try:
            cells.append(types.CellType(c.cell_contents))
        except ValueError:
            cells.append(c)
    return types.FunctionType(fn.__code__, fn.__globals__, fn.__name__, fn.__defaults__, tuple(cells))


class Buf:
    __slots__ = ("name", "w", "r")

    def __init__(self, name):
        self.name = name
        self.w = {}
        self.r = {}


class EngState:
    def __init__(self, name, sem):
        self.name = name
        self.sem = sem
        self.count = 0
        self.waited = {}
        self.ops = []


class KB:
    NSLOT = 8

    def __init__(self, nc, sems):
        self.nc = nc
        self.sems = sems
        it = iter(sems)
        self.eng = {n: EngState(n, next(it)) for n in ("pe", "act", "dve", "pool", "sp")}
        self.slots = {q: [next(it) for _ in range(self.NSLOT)] for q in ("sp", "pool", "act")}
        self.ndma = {q: 0 for q in self.slots}
        self.cc_sem = next(it)
        self.ncc = 0
        self.bufs = {}

    def buf(self, key):
        b = self.bufs.get(key)
        if b is None:
            b = self.bufs[key] = Buf(key)
        return b

    def _deps(self, R, W):
        need = {}
        for b in R:
            for s, v in self.buf(b).w.items():
                if need.get(s, 0) < v:
                    need[s] = v
        for b in W:
            bb = self.buf(b)
            for dct in (bb.w, bb.r):
                for s, v in dct.items():
                    if need.get(s, 0) < v:
                        need[s] = v
        return need

    def _mark(self, R, W, sem, val):
        for b in R:
            self.buf(b).r[sem] = val
        for b in W:
            bb = self.buf(b)
            bb.w[sem] = val

    def _waits(self, es, need, skip_self=False):
        out = []
        for s, v in need.items():
            if skip_self and s is es.sem:
                continue
            if es.waited.get(s, 0) < v:
                es.waited[s] = v
                out.append((s, v))
        return out

    def op(self, e, fn, R=(), W=(), chain=False):
        es = self.eng[e]
        need = self._deps(R, W)
        waits = self._waits(es, need, skip_self=chain)
        es.count += 1
        es.ops.append((waits, _freeze(fn), es.sem, 1))
        self._mark(R, W, es.sem, es.count)

    def dma(self, q, out, in_, R=(), W=(), **kw):
        es = self.eng[q]
        i = self.ndma[q]
        self.ndma[q] += 1
        slot = self.slots[q][i % self.NSLOT]
        gen = i // self.NSLOT
        need = self._deps(R, W)
        if gen > 0:
            if need.get(slot, 0) < 16 * gen:
                need[slot] = 16 * gen
        waits = self._waits(es, need)
        es.ops.append((waits, (lambda eng, o=out, i_=in_, k=kw: eng.dma_start(out=o, in_=i_, **k)), slot, 16))
        self._mark(R, W, slot, 16 * (gen + 1))

    def collective(self, kind, op, ins, outs, R=(), W=()):
        es = self.eng["pool"]
        need = self._deps(R, W)
        waits = self._waits(es, need)
        self.ncc += 1
        n = self.ncc
        es.ops.append((waits, (lambda eng: eng.collective_compute(
            kind, op, replica_groups=[list(range(NCORE))], ins=ins, outs=outs)), self.cc_sem, 1))
        self._mark(R, W, self.cc_sem, n)

    def finish(self, out_keys):
        es = self.eng["sp"]
        need = self._deps(out_keys, ())
        for s, v in need.items():
            if es.waited.get(s, 0) < v:
                es.waited[s] = v
                es.ops.append(([(s, v)], None, None, 0))

    def emit(self, block):
        nc = self.nc
        hw = {"pe": (block.tensor, nc.tensor), "act": (block.scalar, nc.scalar), "dve": (block.vector, nc.vector),
              "pool": (block.gpsimd, nc.gpsimd), "sp": (block.sync, nc.sync)}
        for name, (deco, _) in hw.items():
            es = self.eng[name]

            def body(eng, es=es):
                for waits, fn, sem, inc in es.ops:
                    for s, v in waits:
                        eng.wait_ge(s, v)
                    if fn is not None:
                        ins = fn(eng)
                        if inc == 1:
                            ins.then_inc(sem, 1)
                        else:
                            ins.then_inc(sem, inc)
            deco(body)


def _col_view(ap_row, n):
    return ap_row.rearrange("o (k p) -> p (o k)", p=128)


def build_program(nlayers=NLAYERS, dbg=False):
    nc = bass.Bass("TRN2", target_bir_lowering=False)

    def din(name, shape, dt=F32):
        return nc.dram_tensor(name, list(shape), dt, kind="ExternalInput").ap()

    def dscr(name, shape, dt=F32):
        return nc.dram_tensor(name, list(shape), dt)

    x_in = din("x", [L, D]); ctx_in = din("ctx", [CTX, D])
    cond_in = din("cond", [2, D])
    ada_w = din("ada_w", [4, D, 3 * D]); ada_b = din("ada_b", [4, 3 * D])
    npre = din("norm_pre", [4, D]); npost = din("norm_post", [4, D])
    s5_win = din("s5_win", [2, D, 2 * EC])
    s5_lam = din("s5_lam", [2, 2, 64, 64])
    s5_ldt = din("s5_ldt", [2, 1, 64])
    s5_b = din("s5_b", [2, 2, 64, 64 * 16])
    s5_c = din("s5_c", [2, 2, 64, 64 * 16])
    s5_d = din("s5_d", [2, 128, NCC]); s5_bglu = din("s5_bglu", [2, 128, NCC])
    s5_wglu = din("s5_wglu", [2, E, EC]); s5_wout = din("s5_wout", [2, EC, D])
    hg_win = din("hg_win", [2, D, 5 * EC])
    hg_lb = din("hg_lb", [128, 2 * 2 * NCC])
    hg_norm = din("hg_norm", [2, 128, NCC]); hg_wout = din("hg_wout", [2, EC, D])
    cst_ident = din("c_ident", [128, 128]); cst_iota = din("c_iota", [1, 1024])
    cst_rowmask = din("c_rowmask", [128, 8]); cst_sgn = din("c_sgn", [128, 2])
    cst_tri = din("c_tri", [128, 64])
    cst_reset = din("c_reset", [1, 2 * 1024])
    out_x = nc.dram_tensor("out", [L, D], F32, kind="ExternalOutput").ap()

    xs = dscr("xs", [L, D]); xcs = dscr("xcs", [CTX, D])
    modrow = dscr("modrow", [2, 3 * D])
    projT = dscr("projT", [20, 128, S])
    mixT = dscr("mixT", [NCC, 128, S])
    yb_loc = dscr("yb_loc", [EC, S], BF16); yb_all = dscr("yb_all", [E, S], BF16)
    partial = dscr("partial", [S, D]); oxs = dscr("oxs", [S, D])


    sems = [nc.alloc_semaphore(f"ks{i}") for i in range(5 + 3 * KB.NSLOT + 1)]
    K = KB(nc, sems)
    _uid = [0]

    def SBT(name, shape, dt=F32):
        _uid[0] += 1
        return nc.sbuf_tensor(f"{name}__L{_uid[0]}", shape, dt)

    def kn(t):
        return t.name.split("__L")[0]
    from contextlib import ExitStack

    snaps = []

    def snapshot(name, ap, shape, dt, keys):
        if not dbg:
            return
        t = nc.dram_tensor("d_" + name, list(shape), dt, kind="ExternalOutput").ap()
        K.dma("sp", t, ap, R=keys, W=[("snap", name)])
        snaps.append(("snap", name))

    def barrier():
        ev = {}
        for es in K.eng.values():
            if es.count:
                ev[es.sem] = es.count
        for q, n in K.ndma.items():
            for i in range(max(0, n - K.NSLOT), n):
                ev[K.slots[q][i % K.NSLOT]] = 16 * (i // K.NSLOT + 1)
        if K.ncc:
            ev[K.cc_sem] = K.ncc
        for es in K.eng.values():
            w = K._waits(es, dict(ev))
            if w:
                es.ops.append((w, None, None, 0))

    ps = [nc.alloc_psum_tensor(f"ps{i}", [128, 512], F32) for i in range(8)]

    def PS(i):
        return ("ps", i)

    ident_f = nc.alloc_sbuf_tensor("ident_f", [128, 128], F32)
    ident_b = nc.alloc_sbuf_tensor("ident_b", [128, 128], BF16)
    iota = nc.alloc_sbuf_tensor("iota", [128, 1024], F32)
    iotan = nc.alloc_sbuf_tensor("iotan", [128, 1024], F32)
    rowmask = nc.alloc_sbuf_tensor("rowmask", [128, 8], F32)
    sgn = nc.alloc_sbuf_tensor("sgn", [128, 2], F32)
    scT = nc.alloc_sbuf_tensor("scT", [128, 16, 2], F32)
    modcols = nc.alloc_sbuf_tensor("modcols", [128, 2, 3, 16], F32)
    pre_a = nc.alloc_sbuf_tensor("pre_a", [128, 2, 16], F32)
    gg_bc = nc.alloc_sbuf_tensor("gg_bc", [128, 2, D], F32)
    halfpi = nc.alloc_sbuf_tensor("halfpi", [128, 1], F32)

    K.dma("sp", ident_f[:], cst_ident, W=["ident_f"])
    K.dma("pool", ident_b[:], cst_ident, W=["ident_b"])
    K.dma("sp", iota[:], cst_iota.partition_broadcast(128), W=["iota"])
    K.dma("sp", rowmask[:], cst_rowmask, W=["rowmask"])
    K.dma("sp", sgn[:], cst_sgn, W=["sgn"])
    K.op("pool", lambda e: e.memset(halfpi[:], float(np.pi / 2)), W=["halfpi"])
    K.op("dve", lambda e: e.tensor_scalar(out=iotan[:], in0=iota[:], scalar1=-1.0, scalar2=None, op0=ALU.mult),
         R=["iota"], W=["iotan"])
    with nc.allow_non_contiguous_dma("tiny column-layout loads"):
        for c_ in range(2):
            K.dma("sp", scT[:, :, c_], cond_in[c_:c_ + 1, :].rearrange("o (k p) -> p (o k)", p=128), W=["scT"], allow_slow_non_contiguous=True)
    K.op("act", lambda e: e.activation(out=scT[:], in_=scT[:], func=AF.Silu), R=["scT"], W=["scT"])

    def seq_rows(src_x, src_c, tt, col_major):
        if tt < 2:
            return src_c[tt * 128:(tt + 1) * 128, :]
        if not col_major:
            return src_x[(tt - 2) * 128:(tt - 1) * 128, :]
        return src_x.rearrange("(r w) d -> w r d", w=GRID_W)[tt - 2]

    def rstd_of(e_tile, key, ss, junk, n=128):
        K.op("pool", lambda e: e.memset(ss[:, 0:1], 0.0), W=["ss"])
        K.op("act", lambda e: e.activation(out=junk, in_=e_tile, func=AF.Square, accum_out=ss[:, 0:1]),
             R=[key], W=["junk", "ss"])
        K.op("dve", lambda e: e.tensor_scalar(out=ss[:, 1:2], in0=ss[:, 0:1], scalar1=1.0 / D, scalar2=EPS,
                                              op0=ALU.mult, op1=ALU.add), R=["ss"], W=["ss"])
        K.op("act", lambda e: e.activation(out=ss[:, 2:3], in_=ss[:, 1:2], func=AF.Sqrt), R=["ss"], W=["ss"])
        K.op("dve", lambda e: e.reciprocal(out=ss[:, 3:4], in_=ss[:, 2:3]), R=["ss"], W=["ss"])

    blocks = [(b * 512, 512) for b in range(16)] + [(8192, 256)]

    def outproj(aT, akey, s0, nb, wout_bf, ostage, oi):
        for ts in range(nb // 128):
            og = ostage[oi[0] % 2]; ogk = ("ostage", oi[0] % 2)
            oi[0] += 1
            for nq in range(4):
                pb = 4 + (nq % 2)
                for oc in range(NCC):
                    K.op("pe", lambda e, pb=pb, oc=oc, ts=ts, nq=nq: e.matmul(
                        ps[pb][:, :], lhsT=aT[:, oc, ts * 128:(ts + 1) * 128], rhs=wout_bf[:, oc, nq * 512:(nq + 1) * 512],
                        start=(oc == 0), stop=(oc == NCC - 1)), R=[akey, "wout_bf"], W=[PS(pb)], chain=(oc > 0))
                if nq % 2 == 0:
                    K.op("act", lambda e, pb=pb, nq=nq, og=og: e.activation(out=og[:, nq * 512:(nq + 1) * 512], in_=ps[pb][:, :], func=AF.Identity),
                         R=[PS(pb)], W=[ogk])
                else:
                    K.op("dve", lambda e, pb=pb, nq=nq, og=og: e.tensor_copy(out=og[:, nq * 512:(nq + 1) * 512], in_=ps[pb][:, :]),
                         R=[PS(pb)], W=[ogk])
            tt = s0 // 128 + ts
            K.dma("sp", partial[tt * 128:(tt + 1) * 128, :], og[:], R=[ogk], W=[("partial", tt)])

    def s5_mixer(li, j):
        with ExitStack() as st:
            def sb(name, shape, dt=F32):
                return st.enter_context(SBT(name, shape, dt))
            stp = ExitStack()
            def sbp(name, shape, dt=F32):
                return stp.enter_context(SBT(name, shape, dt))
            dcol = sb("dcol", [128, NCC]); bgl = sb("bgl", [128, NCC])
            BSp = sbp("BSp", [128, 64, 128], BF16); BWp = sbp("BWp", [128, 64, 128], BF16)
            CAp = sbp("CAp", [128, 64, 128], BF16); CBp = sbp("CBp", [128, 64, 128], BF16)
            RHO = sbp("RHO", [128, 64]); THP = sbp("THP", [128, 64])
            K.dma("sp", dcol[:], s5_d[j], W=["dcol"]); K.dma("sp", bgl[:], s5_bglu[j], W=["bgl"])
            with ExitStack() as st2:
                def sb2(name, shape, dt=F32):
                    return st2.enter_context(SBT(name, shape, dt))
                LR = sb2("LR", [128, 64]); LI = sb2("LI", [128, 64]); DT = sb2("DT", [128, 64])
                T = [sb2(f"T{i}", [128, 64]) for i in range(10)]
                Pt = sb2("Pt", [128, 64, 16]); Qt = sb2("Qt", [128, 64, 16])
                BS = sb2("BS", [128, 64, 16]); BW = sb2("BW", [128, 64, 16]); TMP = sb2("TMP", [128, 64, 16])
                for h in range(2):
                    K.dma("sp", LR[h * 64:(h + 1) * 64, :], s5_lam[j, 0], W=["LR"])
                    K.dma("sp", LI[h * 64:(h + 1) * 64, :], s5_lam[j, 1], W=["LI"])
                    K.dma("sp", Pt[h * 64:(h + 1) * 64].rearrange("p a b -> p (a b)"), s5_b[j, h], W=["Pt"])
                    K.dma("sp", Qt[h * 64:(h + 1) * 64].rearrange("p a b -> p (a b)"), s5_b[j, 1 - h], W=["Qt"])
                K.dma("sp", DT[:], s5_ldt[j].partition_broadcast(128), W=["DT"])

                def ew(eng, fn, R, W):
                    K.op(eng, fn, R=R, W=W)
                def tt_(o, a, b, op, eng="dve"):
                    ew(eng, lambda e: e.tensor_tensor(out=o[:], in0=a[:], in1=b[:], op=op), [kn(a), kn(b)], [kn(o)])
                def act_(o, a, func, **kw):
                    ew("act", lambda e: e.activation(out=o[:], in_=a[:], func=func, **kw), [kn(a)], [kn(o)])
                def ts_(o, a, s1, s2, op0, op1=None):
                    if op1 is None:
                        ew("dve", lambda e: e.tensor_scalar(out=o[:], in0=a[:], scalar1=s1, scalar2=None, op0=op0), [kn(a)], [kn(o)])
                    else:
                        ew("dve", lambda e: e.tensor_scalar(out=o[:], in0=a[:], scalar1=s1, scalar2=s2, op0=op0, op1=op1), [kn(a)], [kn(o)])
                act_(DT, DT, AF.Exp)
                tt_(T[0], LR, DT, ALU.mult)
                act_(RHO, T[0], AF.Exp)
                tt_(T[1], LI, DT, ALU.mult)
                ts_(THP, T[1], 1.0 / TWO_PI, None, ALU.mult)
                ts_(T[2], THP, MAGIC, MAGIC, ALU.add, ALU.subtract)
                tt_(T[2], THP, T[2], ALU.subtract)
                ew("dve", lambda e: e.scalar_tensor_tensor(out=T[3][:], in0=T[2][:], scalar=-1.0, in1=T[2][:], op0=ALU.mult, op1=ALU.max), [kn(T[2])], [kn(T[3])])
                act_(T[4], T[2], AF.Sin, scale=TWO_PI)
                act_(T[5], T[3], AF.Sin, scale=-TWO_PI, bias=halfpi[:, 0:1])
                tt_(T[4], T[4], RHO, ALU.mult)
                tt_(T[5], T[5], RHO, ALU.mult)
                ts_(T[5], T[5], -1.0, None, ALU.add)
                tt_(T[6], LR, LR, ALU.mult); tt_(T[7], LI, LI, ALU.mult); tt_(T[6], T[6], T[7], ALU.add)
                ew("dve", lambda e: e.reciprocal(out=T[6][:], in_=T[6][:]), [kn(T[6])], [kn(T[6])])
                tt_(T[7], T[5], LR, ALU.mult); tt_(T[8], T[4], LI, ALU.mult); tt_(T[7], T[7], T[8], ALU.add)
                tt_(T[7], T[7], T[6], ALU.mult)
                tt_(T[8], T[4], LR, ALU.mult); tt_(T[9], T[5], LI, ALU.mult); tt_(T[8], T[8], T[9], ALU.subtract)
                tt_(T[8], T[8], T[6], ALU.mult)
                QR, QI = T[7], T[8]
                ew("dve", lambda e: e.tensor_scalar(out=T[0][:], in0=QI[:], scalar1=sgn[:, 0:1], scalar2=None, op0=ALU.mult), [kn(QI), "sgn"], [kn(T[0])])
                ew("dve", lambda e: e.tensor_scalar(out=T[1][:], in0=QR[:], scalar1=sgn[:, 1:2], scalar2=None, op0=ALU.mult), [kn(QR), "sgn"], [kn(T[1])])
                def bc(t):
                    return t[:].unsqueeze(2).to_broadcast([128, 64, 16])
                ew("dve", lambda e: e.tensor_tensor(out=BS[:], in0=Pt[:], in1=bc(QR), op=ALU.mult), ["Pt", kn(QR)], ["BS"])
                ew("dve", lambda e: e.tensor_tensor(out=TMP[:], in0=Qt[:], in1=bc(T[0]), op=ALU.mult), ["Qt", kn(T[0])], ["TMP"])
                ew("dve", lambda e: e.tensor_tensor(out=BS[:], in0=BS[:], in1=TMP[:], op=ALU.add), ["BS", "TMP"], ["BS"])
                ew("dve", lambda e: e.tensor_tensor(out=BW[:], in0=Qt[:], in1=bc(T[1]), op=ALU.mult), ["Qt", kn(T[1])], ["BW"])
                ew("dve", lambda e: e.tensor_tensor(out=TMP[:], in0=Pt[:], in1=bc(QI), op=ALU.mult), ["Pt", kn(QI)], ["TMP"])
                ew("dve", lambda e: e.tensor_tensor(out=BW[:], in0=BW[:], in1=TMP[:], op=ALU.add), ["BW", "TMP"], ["BW"])
                ti = 0
                for src, dstp, nm in ((BS, BSp, "BSp"), (BW, BWp, "BWp")):
                    for blk in range(8):
                        pb = ti % 2; ti += 1
                        K.op("pe", lambda e, src=src, blk=blk, pb=pb: e.transpose(
                            ps[pb][:, 0:128], src[:, blk * 8:(blk + 1) * 8, :].rearrange("p a b -> p (a b)"), ident_f[:]),
                            R=[kn(src), "ident_f"], W=[PS(pb)])
                        for g in range(8):
                            K.op("dve" if g % 2 == 0 else "act",
                                 (lambda e, dstp=dstp, blk=blk, g=g, pb=pb: e.tensor_scalar(out=dstp[:, blk * 8 + g, :], in0=ps[pb][:, 0:128],
                                                                                      scalar1=rowmask[:, g:g + 1], scalar2=None, op0=ALU.mult)) if g % 2 == 0 else
                                 (lambda e, dstp=dstp, blk=blk, g=g, pb=pb: e.activation(out=dstp[:, blk * 8 + g, :], in_=ps[pb][:, 0:128],
                                                                                    func=AF.Identity, scale=rowmask[:, g:g + 1])),
                                 R=[PS(pb), "rowmask"], W=[nm])
                for h in range(2):
                    K.dma("sp", Pt[h * 64:(h + 1) * 64].rearrange("p a b -> p (a b)"), s5_c[j, h], R=["BS", "BW"], W=["Pt"])
                    K.dma("sp", Qt[h * 64:(h + 1) * 64].rearrange("p a b -> p (a b)"), s5_c[j, 1 - h], R=["BS", "BW"], W=["Qt"])
                K.op("pool", lambda e: e.memset(CAp[:].rearrange("p a b -> p (a b)"), 0.0), W=["CAp"])
                K.op("pool", lambda e: e.memset(CBp[:].rearrange("p a b -> p (a b)"), 0.0), W=["CBp"])
                CA4 = CAp[:].rearrange("p (b g) c -> p b g c", g=8); CB4 = CBp[:].rearrange("p (b g) c -> p b g c", g=8)
                P4 = Pt[:].rearrange("p (b g) c -> p b g c", g=8); Q4 = Qt[:].rearrange("p (b g) c -> p b g c", g=8)
                for g in range(8):
                    K.op("dve", lambda e, g=g: e.tensor_scalar(out=CA4[:, :, g, g * 16:(g + 1) * 16], in0=P4[:, :, g, :], scalar1=sgn[:, 1:2],
                                                           scalar2=None, op0=ALU.mult), R=["Pt", "sgn"], W=["CAp"])
                    K.op("dve", lambda e, g=g: e.tensor_scalar(out=CB4[:, :, g, g * 16:(g + 1) * 16], in0=Q4[:, :, g, :], scalar1=-1.0,
                                                           scalar2=None, op0=ALU.mult), R=["Qt"], W=["CBp"])
            barrier()
            with ExitStack() as st2:
                def sb2(name, shape, dt=F32):
                    return st2.enter_context(SBT(name, shape, dt))
                yacc = sb2("yacc", [128, S])
                ubf = [sb2(f"ubf{i}", [128, 1024], BF16) for i in range(2)]
                WA = [sb2(f"WA{i}", [128, 1024]) for i in range(2)]; WB = [sb2(f"WB{i}", [128, 1024]) for i in range(2)]
                SN = [sb2(f"SN{i}", [128, 1024]) for i in range(2)]; CN = [sb2(f"CN{i}", [128, 1024]) for i in range(2)]
                G1 = [sb2(f"G1{i}", [128, 1024], BF16) for i in range(2)]; G2 = [sb2(f"G2{i}", [128, 1024], BF16) for i in range(2)]
                carry = sb2("carry", [128, 8])
                u32 = [sb2(f"u32{i}", [128, 1024]) for i in range(2)]
                ybt = [sb2(f"ybt{i}", [128, 1024], BF16) for i in range(2)]
                lat = [(CTX + 1024 * i, 1024) for i in range(8)]
                it = 0
                ui = 0
                for cc in range(NCC):
                    for d in range(2):
                        segs = [(0, CTX)] + (lat if d == 0 else lat[::-1])
                        tau0 = 0
                        for si, (s0, n) in enumerate(segs):
                            U = ubf[ui % 2]; uk = ("ubf", ui % 2); ui += 1
                            K.dma("pool", U[:, 0:n], projT[cc, :, s0:s0 + n], R=[("projT", cc, b) for b in range(17)], W=[uk])
                            for g in range(8):
                                gd = d * 32 + cc * 8 + g
                                p = it % 2; it += 1
                                A, B_, Sn, Cn, g1, g2 = WA[p], WB[p], SN[p], CN[p], G1[p], G2[p]
                                kA, kB, kS, kC, k1, k2 = ("WA", p), ("WB", p), ("SN", p), ("CN", p), ("G1", p), ("G2", p)
                                for c0 in range(0, n, 512):
                                    cw = min(512, n - c0)
                                    K.op("pe", lambda e, c0=c0, cw=cw, gd=gd, U=U: e.matmul(ps[c0 // 512][:, 0:cw], lhsT=BSp[:, gd, :], rhs=U[:, c0:c0 + cw], start=True, stop=True),
                                         R=["BSp", uk], W=[PS(c0 // 512)])
                                    K.op("pe", lambda e, c0=c0, cw=cw, gd=gd, U=U: e.matmul(ps[2 + c0 // 512][:, 0:cw], lhsT=BWp[:, gd, :], rhs=U[:, c0:c0 + cw], start=True, stop=True),
                                         R=["BWp", uk], W=[PS(2 + c0 // 512)])
                                if d == 0:
                                    K.op("dve", lambda e, A=A, n=n, tau0=tau0, gd=gd: e.tensor_scalar(out=A[:, 0:n], in0=iota[:, 0:n], scalar1=float(tau0), scalar2=THP[:, gd:gd + 1], op0=ALU.add, op1=ALU.mult),
                                         R=["iota", "THP"], W=[kA])
                                else:
                                    K.op("dve", lambda e, A=A, n=n, tau0=tau0, gd=gd: e.tensor_scalar(out=A[:, 0:n], in0=iotan[:, 0:n], scalar1=float(tau0 + n - 1), scalar2=THP[:, gd:gd + 1], op0=ALU.add, op1=ALU.mult),
                                         R=["iotan", "THP"], W=[kA])
                                K.op("dve", lambda e, A=A, B_=B_, n=n: e.tensor_scalar(out=B_[:, 0:n], in0=A[:, 0:n], scalar1=MAGIC, scalar2=MAGIC, op0=ALU.add, op1=ALU.subtract), R=[kA], W=[kB])
                                K.op("pool", lambda e, A=A, B_=B_, n=n: e.tensor_tensor(out=A[:, 0:n], in0=A[:, 0:n], in1=B_[:, 0:n], op=ALU.subtract), R=[kA, kB], W=[kA])
                                K.op("act", lambda e, A=A, B_=B_, n=n: e.activation(out=B_[:, 0:n], in_=A[:, 0:n], func=AF.Abs), R=[kA], W=[kB])
                                K.op("act", lambda e, A=A, Sn=Sn, n=n: e.activation(out=Sn[:, 0:n], in_=A[:, 0:n], func=AF.Sin, scale=TWO_PI), R=[kA], W=[kS])
                                K.op("act", lambda e, B_=B_, Cn=Cn, n=n: e.activation(out=Cn[:, 0:n], in_=B_[:, 0:n], func=AF.Sin, scale=-TWO_PI, bias=halfpi[:, 0:1]), R=[kB, "halfpi"], W=[kC])
                                for c0 in range(0, n, 512):
                                    cw = min(512, n - c0)
                                    K.op("dve", lambda e, c0=c0, cw=cw, A=A, Cn=Cn: e.tensor_tensor(out=A[:, c0:c0 + cw], in0=ps[c0 // 512][:, 0:cw], in1=Cn[:, c0:c0 + cw], op=ALU.mult),
                                         R=[PS(c0 // 512), kC], W=[kA])
                                    K.op("dve", lambda e, c0=c0, cw=cw, B_=B_, Sn=Sn: e.tensor_tensor(out=B_[:, c0:c0 + cw], in0=ps[2 + c0 // 512][:, 0:cw], in1=Sn[:, c0:c0 + cw], op=ALU.mult),
                                         R=[PS(2 + c0 // 512), kS], W=[kB])
                                K.op("pool", lambda e, A=A, B_=B_, n=n: e.tensor_tensor(out=A[:, 0:n], in0=A[:, 0:n], in1=B_[:, 0:n], op=ALU.add), R=[kA, kB], W=[kA])
                                sl = slice(0, n) if d == 0 else slice(n - 1, None, -1)
                                if d == 0:
                                    vw = lambda t, n=n: t[:, 0:n]
                                else:
                                    vw = lambda t, n=n: t[:, 0:n][:, ::-1]
                                init = 0.0 if si == 0 else carry[:, g:g + 1]
                                K.op("dve", lambda e, A=A, B_=B_, vw=vw, gd=gd, n=n, init=init: e.tensor_tensor_scan(
                                    out=vw(B_), data0=RHO[:, gd:gd + 1].to_broadcast([128, n]), data1=vw(A), initial=init, op0=ALU.mult, op1=ALU.add),
                                    R=[kA, "RHO", "carry"], W=[kB])
                                last = n - 1 if d == 0 else 0
                                K.op("act", lambda e, B_=B_, g=g, last=last: e.activation(out=carry[:, g:g + 1], in_=B_[:, last:last + 1], func=AF.Identity), R=[kB], W=["carry"])
                                K.op("pool", lambda e, g1=g1, B_=B_, Cn=Cn, n=n: e.tensor_tensor(out=g1[:, 0:n], in0=B_[:, 0:n], in1=Cn[:, 0:n], op=ALU.mult), R=[kB, kC], W=[k1])
                                K.op("dve", lambda e, g2=g2, B_=B_, Sn=Sn, n=n: e.tensor_tensor(out=g2[:, 0:n], in0=B_[:, 0:n], in1=Sn[:, 0:n], op=ALU.mult), R=[kB, kS], W=[k2])
                                for c0 in range(0, n, 512):
                                    cw = min(512, n - c0)
                                    pb = 4 + c0 // 512
                                    K.op("pe", lambda e, pb=pb, c0=c0, cw=cw, gd=gd, g1=g1, g=g: e.matmul(ps[pb][:, 0:cw], lhsT=CAp[:, gd, :], rhs=g1[:, c0:c0 + cw], start=(g == 0), stop=False),
                                         R=["CAp", k1], W=[PS(pb)], chain=(g > 0))
                                    K.op("pe", lambda e, pb=pb, c0=c0, cw=cw, gd=gd, g2=g2, g=g: e.matmul(ps[pb][:, 0:cw], lhsT=CBp[:, gd, :], rhs=g2[:, c0:c0 + cw], start=False, stop=(g == 7)),
                                         R=["CBp", k2], W=[PS(pb)], chain=True)
                            for c0 in range(0, n, 512):
                                cw = min(512, n - c0)
                                pb = 4 + c0 // 512
                                if d == 0:
                                    K.op("act", lambda e, pb=pb, c0=c0, cw=cw, s0=s0: e.activation(out=yacc[:, s0 + c0:s0 + c0 + cw], in_=ps[pb][:, 0:cw], func=AF.Identity),
                                         R=[PS(pb)], W=[("yacc", s0)])
                                else:
                                    K.op("dve", lambda e, pb=pb, c0=c0, cw=cw, s0=s0: e.tensor_tensor(out=yacc[:, s0 + c0:s0 + c0 + cw], in0=ps[pb][:, 0:cw], in1=yacc[:, s0 + c0:s0 + c0 + cw], op=ALU.add),
                                         R=[PS(pb), ("yacc", s0)], W=[("yacc", s0)])
                            tau0 += n
                    for fi, (s0, n) in enumerate([(0, CTX)] + lat):
                        U3 = u32[fi % 2]; k3 = ("u32", fi % 2); YB = ybt[fi % 2]; kyb = ("ybt", fi % 2)
                        K.dma("sp", U3[:, 0:n], projT[cc, :, s0:s0 + n], W=[k3])
                        K.op("dve", lambda e, U3=U3, s0=s0, n=n, cc=cc: e.scalar_tensor_tensor(out=U3[:, 0:n], in0=U3[:, 0:n], scalar=dcol[:, cc:cc + 1], in1=yacc[:, s0:s0 + n], op0=ALU.mult, op1=ALU.add),
                             R=[k3, "dcol", ("yacc", s0)], W=[k3])
                        K.op("act", lambda e, U3=U3, n=n: e.activation(out=U3[:, 0:n], in_=U3[:, 0:n], func=AF.Gelu), R=[k3], W=[k3])
                        K.op("pool", lambda e, U3=U3, YB=YB, n=n: e.tensor_copy(out=YB[:, 0:n], in_=U3[:, 0:n]), R=[k3], W=[kyb])
                        K.dma("sp", mixT[cc, :, s0:s0 + n], U3[:, 0:n], R=[k3], W=[("mixT", cc, s0)])
                        K.dma("act", yb_loc[cc * 128:(cc + 1) * 128, s0:s0 + n], YB[:, 0:n], R=[kyb], W=[("ybl", cc, s0)])
            stp.close()
            ykeys = [("ybl", cc, s0) for cc in range(NCC) for s0 in [0] + [CTX + 1024 * i for i in range(8)]]
            K.collective("AllGather", ALU.bypass, [yb_loc.ap().opt()], [yb_all.ap().opt()], R=ykeys, W=["yb_all"])
            barrier()
            with ExitStack() as st2:
                def sb2(name, shape, dt=F32):
                    return st2.enter_context(SBT(name, shape, dt))
                wglu_bf = sb2("wglu_bf", [128, 32, EC], BF16); wout_bf = sb2("wout_bf", [128, NCC, D], BF16)
                ya = [sb2(f"ya{i}", [128, 32, 512], BF16) for i in range(2)]
                sg = [sb2(f"sg{i}", [128, 512]) for i in range(2)]; y3 = [sb2(f"y3{i}", [128, 512]) for i in range(2)]
                z3 = [sb2(f"z3{i}", [128, 512]) for i in range(2)]
                aT = [sb2(f"aT{i}", [128, NCC, 512], BF16) for i in range(2)]
                ostage = [sb2(f"ostage{i}", [128, D]) for i in range(2)]
                for k in range(32):
                    K.dma("pool", wglu_bf[:, k, :], s5_wglu[j, k * 128:(k + 1) * 128, :], W=[("wglu", k)])
                for k in range(NCC):
                    K.dma("pool", wout_bf[:, k, :], s5_wout[j, k * 128:(k + 1) * 128, :], W=["wout_bf"])
                oi = [0]; ei = 0
                yall3 = yb_all[:, :].rearrange("(k p) s -> p k s", p=128)
                for bi, (s0, nb) in enumerate(blocks):
                    Y = ya[bi % 2]; yk = ("ya", bi % 2); AT = aT[bi % 2]; ak = ("aT", bi % 2)
                    for kq in range(4):
                        K.dma("sp" if kq % 2 == 0 else "act", Y[:, kq * 8:(kq + 1) * 8, 0:nb], yall3[:, kq * 8:(kq + 1) * 8, s0:s0 + nb], R=["yb_all"], W=[yk])
                    for oc in range(NCC):
                        pb = oc % 4
                        for k in range(32):
                            K.op("pe", lambda e, pb=pb, oc=oc, k=k, Y=Y, nb=nb: e.matmul(ps[pb][:, 0:nb], lhsT=wglu_bf[:, k, oc * 128:(oc + 1) * 128], rhs=Y[:, k, 0:nb], start=(k == 0), stop=(k == 31)),
                                 R=[("wglu", k), yk], W=[PS(pb)], chain=(k > 0))
                        SG = sg[ei % 2]; Y3 = y3[ei % 2]; Z3 = z3[ei % 2]; ks, ky, kz = ("sg", ei % 2), ("y3", ei % 2), ("z3", ei % 2); ei += 1
                        K.op("act", lambda e, pb=pb, SG=SG, nb=nb, oc=oc: e.activation(out=SG[:, 0:nb], in_=ps[pb][:, 0:nb], func=AF.Sigmoid, bias=bgl[:, oc:oc + 1]), R=[PS(pb), "bgl"], W=[ks])
                        K.dma("sp", Y3[:, 0:nb], mixT[oc, :, s0:s0 + nb], R=[("mixT", oc, s) for s in [0] + [CTX + 1024 * i for i in range(8)]], W=[ky])
                        K.dma("act", Z3[:, 0:nb], projT[NCC + oc, :, s0:s0 + nb], W=[kz])
                        K.op("act", lambda e, Z3=Z3, nb=nb: e.activation(out=Z3[:, 0:nb], in_=Z3[:, 0:nb], func=AF.Silu), R=[kz], W=[kz])
                        K.op("dve", lambda e, Y3=Y3, SG=SG, nb=nb: e.tensor_tensor(out=Y3[:, 0:nb], in0=Y3[:, 0:nb], in1=SG[:, 0:nb], op=ALU.mult), R=[ky, ks], W=[ky])
                        K.op("pool", lambda e, AT=AT, oc=oc, Y3=Y3, Z3=Z3, nb=nb: e.tensor_tensor(out=AT[:, oc, 0:nb], in0=Y3[:, 0:nb], in1=Z3[:, 0:nb], op=ALU.mult), R=[ky, kz], W=[ak])
                    outproj(AT, ak, s0, nb, wout_bf, ostage, oi)

    def hg_mixer(li, j):
        chunks_f = [(c * 32) for c in range(S // 32)]
        chunks_b = [(c * 32) for c in range(CTX // 32 - 1, -1, -1)] + [(c * 32) for c in range(S // 32 - 1, CTX // 32 - 1, -1)]
        pblocks = [(0, CTX)] + [(CTX + 1024 * i, 1024) for i in range(8)]
        with ExitStack() as st:
            def sb(name, shape, dt=F32):
                return st.enter_context(SBT(name, shape, dt))
            LB = sb("LB", [128, 8]); OML = sb("OML", [128, 8]); lbr = sb("lbr", [128, 16]); onrm = sb("onrm", [128, NCC])
            ones = sb("ones", [128, 128]); tri = sb("tri", [128, 64]); rmask = sb("rmask", [128, 2048])
            QtT = sb("QtT", [128, S], BF16); KtT = sb("KtT", [128, S], BF16); vT = sb("vT", [128, S], BF16)
            Ktok = sb("Ktok", [128, 88, 128], BF16); Vtok = sb("Vtok", [128, 88, 128], BF16)
            eBl = sb("eBl", [128, S // 32]); oacc = sb("oacc", [128, S])
            TT = [sb(f"hT{i}", [128, 1024]) for i in range(6)]
            S32 = sb("S32", [128, 128]); Sbf = sb("Sbf", [128, 128], BF16)
            scm = [sb(f"scm{i}", [128, 32], BF16) for i in range(2)]
            K.dma("sp", lbr[:], hg_lb, W=["lbr"]); K.dma("sp", onrm[:], hg_norm[j], W=["onrm"])
            K.dma("sp", tri[:], cst_tri, W=["tri"]); K.dma("sp", rmask[:], cst_reset.partition_broadcast(128), W=["rmask"])
            K.op("pool", lambda e: e.memset(ones[:], 1.0), W=["ones"])
            lb3 = lbr[:].rearrange("p (d l c) -> p d l c", d=2, l=2)
            LB3 = LB[:].rearrange("p (d c) -> p d c", d=2); OML3 = OML[:].rearrange("p (d c) -> p d c", d=2)
            if j == 0:
                K.op("pool", lambda e: e.memset(LB[:], 0.0), W=["LB"])
            else:
                K.op("dve", lambda e: e.tensor_tensor(out=LB3, in0=lb3[:, :, 1, :], in1=lb3[:, :, 0, :], op=ALU.subtract), R=["lbr"], W=["LB"])
                K.op("act", lambda e: e.activation(out=LB[:], in_=LB[:], func=AF.Sigmoid), R=["LB"], W=["LB"])
            K.op("dve", lambda e: e.tensor_scalar(out=OML[:], in0=LB[:], scalar1=-1.0, scalar2=1.0, op0=ALU.mult, op1=ALU.add), R=["LB"], W=["OML"])
            psb6 = ps[6][:].bitcast(BF16); psb7 = ps[7][:].bitcast(BF16)

            def to_tok(srcT, skey, dst, dkey):
                for t0 in range(0, 88, 8):
                    pv = psb6 if (t0 // 8) % 2 == 0 else psb7
                    pk = PS(6 + (t0 // 8) % 2)
                    for q in range(8):
                        K.op("pe", lambda e, q=q, t0=t0, pv=pv: e.transpose(pv[0:96, q * 128:(q + 1) * 128], srcT[:, (t0 + q) * 96:(t0 + q + 1) * 96], ident_b[:]),
                             R=[skey, "ident_b"], W=[pk])
                    if (t0 // 8) % 2 == 0:
                        K.op("act", lambda e, t0=t0, pv=pv: e.activation(out=dst[0:96, t0:t0 + 8, :].rearrange("p a b -> p (a b)"), in_=pv[0:96, :], func=AF.Identity), R=[pk], W=[dkey])
                    else:
                        K.op("dve", lambda e, t0=t0, pv=pv: e.tensor_copy(out=dst[0:96, t0:t0 + 8, :].rearrange("p a b -> p (a b)"), in_=pv[0:96, :]), R=[pk], W=[dkey])

            for cc in range(NCC):
                for (s0, n) in pblocks:
                    K.dma("pool", vT[:, s0:s0 + n], projT[cc, :, s0:s0 + n], W=["vT"])
                to_tok(vT, "vT", Vtok, "Vtok")
                for d in range(2):
                    dcc = d * NCC + cc
                    for (s0, n) in pblocks:
                        fr, t1, t2, t3, t4, t5 = [t[:, 0:n] for t in TT]
                        vw = (lambda a: a) if d == 0 else (lambda a: a[:, ::-1])
                        K.dma("sp", fr, projT[4 + 4 * d + cc, :, s0:s0 + n], W=["hT0"])
                        K.op("act", lambda e, fr=fr, t1=t1: e.activation(out=t1, in_=fr, func=AF.Sigmoid), R=["hT0"], W=["hT1"])
                        K.op("dve", lambda e, t1=t1, dcc=dcc: e.tensor_scalar(out=t1, in0=t1, scalar1=OML[:, dcc:dcc + 1], scalar2=LB[:, dcc:dcc + 1], op0=ALU.mult, op1=ALU.add),
                             R=["hT1", "OML", "LB"], W=["hT1"])
                        K.op("act", lambda e, t1=t1: e.activation(out=t1, in_=t1, func=AF.Ln), R=["hT1"], W=["hT1"])
                        K.op("act", lambda e, fr=fr, t2=t2: e.activation(out=t2, in_=fr, func=AF.Sigmoid, scale=-1.0), R=["hT0"], W=["hT2"])
                        K.op("dve", lambda e, t2=t2, dcc=dcc: e.tensor_scalar(out=t2, in0=t2, scalar1=OML[:, dcc:dcc + 1], scalar2=None, op0=ALU.mult),
                             R=["hT2", "OML"], W=["hT2"])
                        K.op("dve", lambda e, t1=t1, t3=t3, n=n, vw=vw, d=d: e.tensor_tensor_scan(
                            out=vw(t3), data0=vw(rmask[:, d * 1024:d * 1024 + n]), data1=vw(t1), initial=0.0, op0=ALU.mult, op1=ALU.add),
                            R=["hT1", "rmask"], W=["hT3"])
                        off = 31 if d == 0 else 0
                        K.op("act", lambda e, t3=t3, t4=t4: e.activation(out=t4, in_=t3, func=AF.Exp), R=["hT3"], W=["hT4"])
                        K.op("act", lambda e, t3=t3, t5=t5: e.activation(out=t5, in_=t3, func=AF.Exp, scale=-1.0), R=["hT3"], W=["hT5"])
                        K.op("pool", lambda e, t2=t2, t5=t5, s0=s0, n=n: e.tensor_tensor(out=KtT[:, s0:s0 + n], in0=t2, in1=t5, op=ALU.mult), R=["hT2", "hT5"], W=["KtT"])
                        K.op("pool", lambda e, t4=t4, s0=s0, n=n, off=off: e.tensor_copy(out=eBl[:, s0 // 32:(s0 + n) // 32], in_=t4[:, off::32]), R=["hT4"], W=["eBl"])
                        K.dma("act", fr, projT[12 + cc, :, s0:s0 + n], W=["hT0"])
                        K.op("act", lambda e, fr=fr: e.activation(out=fr, in_=fr, func=AF.Silu), R=["hT0"], W=["hT0"])
                        K.op("dve", lambda e, fr=fr, t4=t4, s0=s0, n=n: e.tensor_tensor(out=QtT[:, s0:s0 + n], in0=fr, in1=t4, op=ALU.mult), R=["hT0", "hT4"], W=["QtT"])
                    to_tok(KtT, "KtT", Ktok, "Ktok")
                    K.op("pool", lambda e: e.memset(S32[:], 0.0), W=["S32"])
                    K.op("pool", lambda e: e.memset(Sbf[:], 0.0), W=["Sbf"])
                    order = chunks_f if d == 0 else chunks_b
                    cur = None
                    span = None
                    bank_i = 0

                    def evac(key, lo, hi, pbk, d=d):
                        base = 0 if key[0] == "c" else key[1] * 512
                        for a in range(lo, hi, 512):
                            b = min(hi, a + 512)
                            if d == 0:
                                K.op("act", lambda e, a=a, b=b: e.activation(out=oacc[:, a:b], in_=ps[pbk][:, a - base:b - base], func=AF.Identity), R=[PS(pbk)], W=[("oacc", a // 512)])
                            else:
                                K.op("dve", lambda e, a=a, b=b: e.tensor_tensor(out=oacc[:, a:b], in0=ps[pbk][:, a - base:b - base], in1=oacc[:, a:b], op=ALU.add), R=[PS(pbk), ("oacc", a // 512)], W=[("oacc", a // 512)])
                    for ci, c0 in enumerate(order):
                        key = ("c", 0) if c0 < CTX else ("l", c0 // 512)
                        if key != cur:
                            if cur is not None:
                                evac(cur, span[0], span[1], 4 + bank_i % 2)
                                bank_i += 1
                            cur = key; span = [c0, c0 + 32]
                        span[0] = min(span[0], c0); span[1] = max(span[1], c0 + 32)
                        pbk = 4 + bank_i % 2
                        base = 0 if key[0] == "c" else key[1] * 512
                        tI, ro = c0 // 96, c0 % 96
                        sc = scm[ci % 2]; sck = ("scm", ci % 2)
                        K.op("pe", lambda e, c0=c0, tI=tI: e.matmul(ps[0][0:96, 0:32], lhsT=KtT[:, tI * 96:(tI + 1) * 96], rhs=QtT[:, c0:c0 + 32], start=True, stop=True),
                             R=["KtT", "QtT"], W=[PS(0)])
                        K.op("dve", lambda e, sc=sc, d=d, ro=ro: e.tensor_tensor(out=sc[ro:ro + 32, :], in0=ps[0][ro:ro + 32, 0:32], in1=tri[ro:ro + 32, d * 32:(d + 1) * 32], op=ALU.mult), R=[PS(0), "tri"], W=[sck])
                        K.op("pe", lambda e, c0=c0, tI=tI, ro=ro, sc=sc, pbk=pbk, base=base: e.matmul(ps[pbk][:, c0 - base:c0 - base + 32], lhsT=Vtok[ro:ro + 32, tI, :], rhs=sc[ro:ro + 32, :], start=True, stop=False),
                             R=["Vtok", sck], W=[PS(pbk)])
                        K.op("pe", lambda e, c0=c0, pbk=pbk, base=base: e.matmul(ps[pbk][:, c0 - base:c0 - base + 32], lhsT=Sbf[:], rhs=QtT[:, c0:c0 + 32], start=False, stop=True),
                             R=["Sbf", "QtT"], W=[PS(pbk)], chain=True)
                        K.op("pe", lambda e, tI=tI, ro=ro: e.matmul(ps[1][:, 0:128], lhsT=Ktok[ro:ro + 32, tI, :], rhs=Vtok[ro:ro + 32, tI, :], start=True, stop=True),
                             R=["Ktok", "Vtok"], W=[PS(1)])
                        cidx = c0 // 32
                        K.op("dve", lambda e: e.tensor_tensor(out=S32[:], in0=ps[1][:, 0:128], in1=S32[:], op=ALU.add), R=[PS(1), "S32"], W=["S32"])
                        K.op("dve", lambda e, cidx=cidx: e.tensor_scalar(out=Sbf[:], in0=S32[:], scalar1=eBl[:, cidx:cidx + 1], scalar2=None, op0=ALU.mult), R=["S32", "eBl"], W=["Sbf"])
                        K.op("act", lambda e, cidx=cidx: e.activation(out=S32[:], in_=S32[:], func=AF.Identity, scale=eBl[:, cidx:cidx + 1]), R=["S32", "eBl"], W=["S32"])
                    evac(cur, span[0], span[1], 4 + bank_i % 2)
                for bi, (s0, nb) in enumerate(blocks):
                    t0_, t1_, t2_ = TT[0][:, 0:nb], TT[1][:, 0:nb], TT[2][:, 0:nb]
                    ob = oacc[:, s0:s0 + nb]
                    K.op("act", lambda e, ob=ob, t0_=t0_: e.activation(out=t0_, in_=ob, func=AF.Square), R=[("oacc", s0 // 512)], W=["hT0"])
                    K.op("pe", lambda e, t0_=t0_, nb=nb: e.matmul(ps[2][:, 0:nb], lhsT=ones[:], rhs=t0_, start=True, stop=True), R=["ones", "hT0"], W=[PS(2)])
                    K.op("dve", lambda e, t1_=t1_, nb=nb: e.tensor_scalar(out=t1_, in0=ps[2][:, 0:nb], scalar1=1.0 / 128, scalar2=EPS, op0=ALU.mult, op1=ALU.add), R=[PS(2)], W=["hT1"])
                    K.op("act", lambda e, t1_=t1_: e.activation(out=t1_, in_=t1_, func=AF.Sqrt), R=["hT1"], W=["hT1"])
                    K.op("dve", lambda e, t1_=t1_: e.reciprocal(out=t1_, in_=t1_), R=["hT1"], W=["hT1"])
                    K.op("dve", lambda e, ob=ob, t1_=t1_, cc=cc: e.scalar_tensor_tensor(out=t1_, in0=ob, scalar=onrm[:, cc:cc + 1], in1=t1_, op0=ALU.mult, op1=ALU.mult),
                         R=[("oacc", s0 // 512), "hT1", "onrm"], W=["hT1"])
                    K.dma("sp", t2_, projT[16 + cc, :, s0:s0 + nb], W=["hT2"])
                    K.op("act", lambda e, t2_=t2_: e.activation(out=t2_, in_=t2_, func=AF.Silu), R=["hT2"], W=["hT2"])
                    ab = TT[3][:, 0:nb // 2].bitcast(BF16) if False else None
                    K.op("pool", lambda e, t1_=t1_, t2_=t2_, nb=nb: e.tensor_tensor(out=KtT[:, 0:nb], in0=t1_, in1=t2_, op=ALU.mult), R=["hT1", "hT2"], W=["KtT"])
                    K.dma("act", yb_loc[cc * 128:(cc + 1) * 128, s0:s0 + nb], KtT[:, 0:nb], R=["KtT"], W=[("ybl", cc, bi)])
            barrier()
        with ExitStack() as st2:
            def sb2(name, shape, dt=F32):
                return st2.enter_context(SBT(name, shape, dt))
            wout_bf = sb2("hwout_bf", [128, NCC, D], BF16)
            aT = [sb2(f"haT{i}", [128, NCC, 512], BF16) for i in range(2)]
            ostage = [sb2(f"hostage{i}", [128, D]) for i in range(2)]
            for k in range(NCC):
                K.dma("pool", wout_bf[:, k, :], hg_wout[j, k * 128:(k + 1) * 128, :], W=["wout_bf"])
            oi = [0]
            yl3 = yb_loc[:, :].rearrange("(k p) s -> p k s", p=128)
            for bi, (s0, nb) in enumerate(blocks):
                AT = aT[bi % 2]; ak = ("aT", bi % 2)
                K.dma("sp", AT[:, :, 0:nb], yl3[:, :, s0:s0 + nb], W=[ak])
                outproj(AT, ak, s0, nb, wout_bf, ostage, oi)

    for li in range(nlayers):
        j = li // 2
        is_s5 = li % 2 == 0
        col_major = j % 2 == 1
        need_ctx = li < 3
        src_x = x_in if li == 0 else xs[:, :]
        src_c = ctx_in if li == 0 else xcs[:, :]
        dst_x = out_x if li == nlayers - 1 else xs[:, :]
        nch = 2 * NCC if is_s5 else 5 * NCC
        win = s5_win[j] if is_s5 else hg_win[j]

        barrier()
        with ExitStack() as st:
            wA = [st.enter_context(SBT(f"wA{t}", [128, 16, 512], F32)) for t in range(2)]
            modsb = st.enter_context(SBT("modsb", [2, 3 * D], F32))
            biasb = st.enter_context(SBT("biasb", [2, 3 * D], F32))
            gpre = st.enter_context(SBT("gpre", [128, 16], F32))
            gbc = st.enter_context(SBT("gbc", [128, D], F32))
            K.dma("sp", biasb[:], ada_b[li:li + 1, :].partition_broadcast(2), W=["biasb"])
            for n in range(12):
                w = wA[n % 2]
                K.dma("sp" if n % 2 == 0 else "act", w[:], ada_w[li].rearrange("(k p) n -> p k n", p=128)[:, :, n * 512:(n + 1) * 512],
                      W=[("wA", n % 2)])
                for k in range(16):
                    K.op("pe", lambda e, w=w, k=k: e.matmul(ps[0][0:2, :], lhsT=scT[:, k, :], rhs=w[:, k, :], start=(k == 0), stop=(k == 15)),
                         R=[("wA", n % 2), "scT"], W=[PS(0)], chain=(k > 0))
                K.op("dve", lambda e, n=n: e.tensor_tensor(out=modsb[:, n * 512:(n + 1) * 512], in0=ps[0][0:2, :],
                                                          in1=biasb[:, n * 512:(n + 1) * 512], op=ALU.add),
                     R=[PS(0), "biasb"], W=["modsb"])
            K.dma("sp", modrow[:, :], modsb[:], R=["modsb"], W=["modrow"])
            with nc.allow_non_contiguous_dma("tiny column-layout loads"):
                for c_ in range(2):
                    for m_ in range(3):
                        K.dma("sp", modcols[:, c_, m_, :], modrow[c_:c_ + 1, m_ * D:(m_ + 1) * D].rearrange("o (k p) -> p (o k)", p=128),
                              R=["modrow"], W=[("modcols", c_, m_)], allow_slow_non_contiguous=True)
                K.dma("sp", gpre[:], npre[li:li + 1, :].rearrange("o (k p) -> p (o k)", p=128), W=["gpre"], allow_slow_non_contiguous=True)
            for c_ in range(2):
                K.dma("sp", gg_bc[:, c_, :], modrow[c_:c_ + 1, 2 * D:3 * D].partition_broadcast(128), R=["modrow"], W=["gg_bc"])
            K.dma("sp", gbc[:], npost[li:li + 1, :].partition_broadcast(128), W=["gbc"])
            for c_ in range(2):
                K.op("pool", lambda e, c_=c_: e.tensor_tensor(out=gg_bc[:, c_, :], in0=gg_bc[:, c_, :], in1=gbc[:], op=ALU.mult),
                     R=["gg_bc", "gbc"], W=["gg_bc"])
                K.op("dve", lambda e, c_=c_: e.scalar_tensor_tensor(out=pre_a[:, c_, :], in0=modcols[:, c_, 1, :], scalar=1.0,
                                                                    in1=gpre[:], op0=ALU.add, op1=ALU.mult),
                     R=[("modcols", c_, 1), "gpre"], W=["pre_a"])

        barrier()
        with ExitStack() as st:
            wbf = st.enter_context(SBT("wbf", [128, 16, nch * 128], BF16))
            hxT = [st.enter_context(SBT(f"hxT{t}", [128, 16, 512], BF16)) for t in range(2)]
            xt = [st.enter_context(SBT(f"xt{t}", [128, D], F32)) for t in range(2)]
            xn = [st.enter_context(SBT(f"xn{t}", [128, D], BF16)) for t in range(2)]
            junk = st.enter_context(SBT("junk", [128, D], BF16))
            ss = st.enter_context(SBT("ss", [128, 4], F32))
            tmpf = st.enter_context(SBT("tmpf", [128, 8, 128], F32))
            stage = [st.enter_context(SBT(f"stage{t}", [128, 512], F32)) for t in range(2)]
            psb = [ps[6][:].bitcast(BF16), ps[7][:].bitcast(BF16)]
            for k in range(16):
                K.dma("pool", wbf[:, k, :], win[k * 128:(k + 1) * 128, :], W=[("wbf", k)])
            ti = 0
            ei = 0
            for bi, (s0, nb) in enumerate(blocks):
                hb = hxT[bi % 2]
                hkey = ("hxT", bi % 2)
                for tl in range(nb // 128):
                    tt = s0 // 128 + tl
                    c_ = 0 if tt >= 2 else 1
                    X = xt[ti % 2]; XN = xn[ti % 2]; xk = ("xt", ti % 2); nk = ("xn", ti % 2)
                    ti += 1
                    K.dma("sp", X[:], seq_rows(src_x, src_c, tt, col_major), W=[xk])
                    rstd_of(X[:], xk, ss, junk[:])
                    K.op("dve", lambda e, X=X, XN=XN: e.tensor_scalar(out=XN[:], in0=X[:], scalar1=ss[:, 3:4], scalar2=None, op0=ALU.mult),
                         R=[xk, "ss"], W=[nk])
                    if li == 0 and tt == 2:
                        snapshot("X", X[:, 0:64], [128, 64], F32, [xk])
                        snapshot("ss", ss[:], [128, 4], F32, ["ss"])
                        snapshot("XN", XN[:, 0:64], [128, 64], BF16, [nk])
                    for h in range(2):
                        for k8 in range(8):
                            k = h * 8 + k8
                            K.op("pe", lambda e, h=h, k8=k8, k=k, XN=XN: e.transpose(psb[h][:, k8 * 128:(k8 + 1) * 128], XN[:, k * 128:(k + 1) * 128], ident_b[:]),
                                 R=[nk, "ident_b"], W=[PS(6 + h)])
                        K.op("dve", lambda e, h=h, c_=c_: e.tensor_tensor(
                            out=tmpf[:], in0=psb[h].rearrange("p (k t) -> p k t", t=128),
                            in1=pre_a[:, c_, h * 8:(h + 1) * 8].unsqueeze(2).to_broadcast([128, 8, 128]), op=ALU.mult),
                            R=[PS(6 + h), "pre_a"], W=["tmpf"])
                        K.op("pool", lambda e, h=h, c_=c_, tl=tl, hb=hb: e.tensor_tensor(
                            out=hb[:, h * 8:(h + 1) * 8, tl * 128:(tl + 1) * 128], in0=tmpf[:],
                            in1=modcols[:, c_, 0, h * 8:(h + 1) * 8].unsqueeze(2).to_broadcast([128, 8, 128]), op=ALU.add),
                            R=["tmpf", ("modcols", c_, 0)], W=[hkey])
                if li == 0 and bi == 0:
                    snapshot("hb", hb[:, :, 256:384], [128, 16, 128], BF16, [hkey])
                    snapshot("pre_a", pre_a[:], [128, 2, 16], F32, ["pre_a"])
                for cc in range(nch):
                    pb = cc % 4
                    for k in range(16):
                        K.op("pe", lambda e, pb=pb, cc=cc, k=k, hb=hb, nb=nb: e.matmul(
                            ps[pb][:, 0:nb], lhsT=wbf[:, k, cc * 128:(cc + 1) * 128], rhs=hb[:, k, 0:nb], start=(k == 0), stop=(k == 15)),
                            R=[("wbf", k), hkey], W=[PS(pb)], chain=(k > 0))
                    sg = stage[ei % 2]; sk = ("stage", ei % 2)
                    if ei % 2 == 0:
                        K.op("act", lambda e, pb=pb, sg=sg, nb=nb: e.activation(out=sg[:, 0:nb], in_=ps[pb][:, 0:nb], func=AF.Identity),
                             R=[PS(pb)], W=[sk])
                    else:
                        K.op("dve", lambda e, pb=pb, sg=sg, nb=nb: e.tensor_copy(out=sg[:, 0:nb], in_=ps[pb][:, 0:nb]), R=[PS(pb)], W=[sk])
                    ei += 1
                    K.dma("sp", projT[cc, :, s0:s0 + nb], sg[:, 0:nb], R=[sk], W=[("projT", cc, bi)])

        barrier()
        if is_s5:
            s5_mixer(li, j)
        else:
            hg_mixer(li, j)

        K.collective("AllReduce", ALU.add, [partial.ap().opt()], [oxs.ap().opt()], R=[("partial", t) for t in range(NT)], W=["oxs"])
        barrier()
        with ExitStack() as st:
            ot = [st.enter_context(SBT(f"ot{t}", [128, D], F32)) for t in range(2)]
            xt = [st.enter_context(SBT(f"xr{t}", [128, D], F32)) for t in range(2)]
            junk = st.enter_context(SBT("junk5", [128, D], BF16))
            ss = st.enter_context(SBT("ss5", [128, 4], F32))
            for tt in range(NT):
                if tt < 2 and not need_ctx:
                    continue
                c_ = 0 if tt >= 2 else 1
                O = ot[tt % 2]; X = xt[tt % 2]; ok = ("ot", tt % 2); xk = ("xr", tt % 2)
                K.dma("sp", O[:], oxs[tt * 128:(tt + 1) * 128, :], R=["oxs"], W=[ok])
                K.dma("act", X[:], seq_rows(src_x, src_c, tt, col_major), R=[("xdst", tt)], W=[xk])
                rstd_of(O[:], ok, ss, junk[:])
                K.op("dve", lambda e, O=O, c_=c_: e.scalar_tensor_tensor(out=O[:], in0=O[:], scalar=ss[:, 3:4], in1=gg_bc[:, c_, :],
                                                                        op0=ALU.mult, op1=ALU.mult), R=[ok, "ss", "gg_bc"], W=[ok])
                K.op("pool", lambda e, O=O, X=X: e.tensor_tensor(out=O[:], in0=O[:], in1=X[:], op=ALU.add), R=[ok, xk], W=[ok])
                dst = seq_rows(dst_x, xcs[:, :], tt, col_major)
                K.dma("sp", dst, O[:], R=[ok], W=[("xdst", tt)])

    fin = [("xdst", tt) for tt in range(NT)]
    if dbg:
        barrier()
        d_proj = nc.dram_tensor("d_proj", [2, 128, 1024], F32, kind="ExternalOutput").ap()
        d_mix = nc.dram_tensor("d_mix", [128, 1024], F32, kind="ExternalOutput").ap()
        d_ox = nc.dram_tensor("d_ox", [1024, D], F32, kind="ExternalOutput").ap()
        d_mod = nc.dram_tensor("d_mod", [2, 3 * D], F32, kind="ExternalOutput").ap()
        d_par = nc.dram_tensor("d_par", [1024, D], F32, kind="ExternalOutput").ap()
        K.dma("sp", d_proj[0], projT[0, :, 0:1024], W=["d0"]); K.dma("sp", d_proj[1], projT[4, :, 0:1024], W=["d1"])
        K.dma("sp", d_mix, mixT[0, :, 0:1024], W=["d2"])
        for q_ in range(8):
            K.dma("sp", d_ox[q_ * 128:(q_ + 1) * 128, :], oxs[q_ * 128:(q_ + 1) * 128, :], W=[("d3", q_)])
            K.dma("sp", d_par[q_ * 128:(q_ + 1) * 128, :], partial[q_ * 128:(q_ + 1) * 128, :], W=[("d5", q_)])
        K.dma("sp", d_mod, modrow[:, :], W=["d4"])
        fin += snaps
        fin += ["d0", "d1", "d2", "d4"] + [("d3", q_) for q_ in range(8)] + [("d5", q_) for q_ in range(8)]
    K.finish(fin)
    with nc.Block() as block:
        K.emit(block)
    return nc


_PROG = {}


def _host_inputs(inp):
    f = np.float32
    g = {k: np.asarray(v, dtype=f) for k, v in inp.items()}
    common = {
        "x": np.ascontiguousarray(g["x"][0]), "ctx": np.ascontiguousarray(g["ctx"][0]),
        "cond": np.ascontiguousarray(np.stack([g["c"][0], g["c_ctx"]], 0)),
        "ada_w": g["ada_w"], "ada_b": g["ada_b"], "norm_pre": g["norm_pre"], "norm_post": g["norm_post"],
        "c_ident": np.eye(128, dtype=f), "c_iota": np.arange(1024, dtype=f)[None, :],
        "c_rowmask": (np.arange(128)[:, None] // 16 == np.arange(8)[None, :]).astype(f),
        "c_sgn": np.stack([np.where(np.arange(128) < 64, -1.0, 1.0), np.where(np.arange(128) < 64, 1.0, -1.0)], 1).astype(f),
        "c_tri": np.tile(np.concatenate([np.triu(np.ones((32, 32), f)), np.tril(np.ones((32, 32), f))], 1), (4, 1)),
        "c_reset": np.concatenate([(np.arange(1024) % 32 != 0), (np.arange(1024) % 32 != 31)]).astype(f)[None, :],
    }
    maps = []
    for k in range(NCORE):
        cs = slice(k * EC, (k + 1) * EC)
        gs = slice(k * 32, (k + 1) * 32)
        m = dict(common)
        m["s5_win"] = np.ascontiguousarray(np.concatenate([g["s5_w_in"][:, :, cs], g["s5_w_in"][:, :, E + k * EC:E + (k + 1) * EC]], 2))
        lam = np.stack([g["s5_lam_re"][:, :, gs, :], g["s5_lam_im"][:, :, gs, :]], 1)
        m["s5_lam"] = np.ascontiguousarray(lam.transpose(0, 1, 4, 2, 3).reshape(2, 2, 64, 64))
        m["s5_ldt"] = np.ascontiguousarray(g["s5_log_dt"][:, :, gs].reshape(2, 1, 64))
        b = np.stack([g["s5_b_re"][:, :, gs], g["s5_b_im"][:, :, gs]], 1)
        m["s5_b"] = np.ascontiguousarray(b.transpose(0, 1, 4, 2, 3, 5).reshape(2, 2, 64, 64 * 16))
        c = np.stack([g["s5_c_re"][:, :, gs], g["s5_c_im"][:, :, gs]], 1)
        m["s5_c"] = np.ascontiguousarray(c.transpose(0, 1, 5, 2, 3, 4).reshape(2, 2, 64, 64 * 16))
        m["s5_d"] = np.ascontiguousarray(g["s5_d"][:, cs].reshape(2, NCC, 128).transpose(0, 2, 1))
        m["s5_bglu"] = np.ascontiguousarray(g["s5_b_glu"][:, cs].reshape(2, NCC, 128).transpose(0, 2, 1))
        m["s5_wglu"] = np.ascontiguousarray(g["s5_w_glu"][:, :, cs])
        m["s5_wout"] = np.ascontiguousarray(g["s5_w_out"][:, cs, :])
        m["hg_win"] = np.ascontiguousarray(np.concatenate([g["hgrn_w_in"][:, :, q * E + k * EC:q * E + (k + 1) * EC] for q in range(5)], 2))
        lb = g["hgrn_lb"][:, :, cs].reshape(2, 2, NCC, 128)
        m["hg_lb"] = np.ascontiguousarray(lb.transpose(3, 0, 1, 2).reshape(128, 2 * 2 * NCC))
        m["hg_norm"] = np.ascontiguousarray(g["hgrn_norm"][:, k * 4:(k + 1) * 4, :].transpose(0, 2, 1))
        m["hg_wout"] = np.ascontiguousarray(g["hgrn_w_out"][:, cs, :])
        maps.append(m)
    return maps


def kernel(**inputs):
    nl = NLAYERS
    if nl not in _PROG:
        _PROG[nl] = build_program(nl)
    nc = _PROG[nl]
    maps = _host_inputs(inputs)
    res = run_bass_kernel_spmd(nc, maps, core_ids=list(range(NCORE)))
    global _LAST
    _LAST = res
    return np.asarray(res.results[0]["out"], dtype=np.float32).reshape(1, L, D)
```
